# Optimizing a Trainium2 kernel written in Bass

```python
import jax, jax.numpy as jnp
from jax import lax
import numpy as np

D_MODEL = 1024
BATCH = 4
SEQ = 4096
DEPTH = 1

HEAD_DIM = 64
HEADS_PER_GROUP = 4
DILATED_GROUPS = ((128, 1), (512, 4), (2048, 16))
N_GROUPS = len(DILATED_GROUPS)
N_ATTN_HEADS = N_GROUPS * HEADS_PER_GROUP
ATTN_WIDTH = N_ATTN_HEADS * HEAD_DIM
ATTN_OUT_WIDTH = HEADS_PER_GROUP * HEAD_DIM
ROT_DIM = HEAD_DIM // 4
ROPE_THETA = 500000.0
BLOCK = 128
POOL_WINDOWS = (2, 4, 8, 16)
POOL_GROUP_WIDTH = 128
POOL_WIDTH = len(POOL_WINDOWS) * POOL_GROUP_WIDTH
N_BRANCHES = 2
IN_WIDTH = 3 * ATTN_WIDTH + POOL_WIDTH + N_BRANCHES * D_MODEL
D_FF = 2816
RMS_EPS = 1e-6

kernel_name = "hybrid_dilated_attn_pool_macaron_block"


def rms_norm(x, g):
    xf = x.astype(jnp.float32)
    y = xf * lax.rsqrt(jnp.mean(xf * xf, axis=-1, keepdims=True) + RMS_EPS)
    return (y * g.astype(jnp.float32)).astype(x.dtype)


def swiglu(x, w1, w3, w2):
    return (jax.nn.silu(x @ w1) * (x @ w3)) @ w2


def rope_tables(seq):
    pos = jnp.arange(seq, dtype=jnp.float32)
    inv = ROPE_THETA ** (-jnp.arange(0, ROT_DIM, 2, dtype=jnp.float32) / ROT_DIM)
    ang = pos[:, None] * inv[None, :]
    return jnp.cos(ang), jnp.sin(ang)


def apply_partial_rope(x, cos, sin):
    xf = x.astype(jnp.float32)
    half = ROT_DIM // 2
    x1, x2 = xf[..., :half], xf[..., half:ROT_DIM]
    c, s = cos[None, :, None, :], sin[None, :, None, :]
    out = jnp.concatenate([x1 * c - x2 * s, x2 * c + x1 * s, xf[..., ROT_DIM:]], axis=-1)
    return out.astype(x.dtype)


def dilated_window_attention(q, k, v, window, dilation):
    B, S, H, Dh = q.shape
    r = dilation
    L = S // r
    n_back = window // r
    nb = -(-L // BLOCK)
    Lp = nb * BLOCK

    def to_blocks(t):
        t = t.reshape(B, L, r, H, Dh).transpose(0, 2, 1, 3, 4)
        t = jnp.pad(t, ((0, 0), (0, 0), (0, Lp - L), (0, 0), (0, 0)))
        return t.reshape(B, r, nb, BLOCK, H, Dh)

    def band(t):
        prev = jnp.pad(t[:, :, :-1], ((0, 0), (0, 0), (1, 0), (0, 0), (0, 0), (0, 0)))
        return jnp.concatenate([prev, t], axis=3)

    qb = to_blocks(q)
    kk = band(to_blocks(k))
    vv = band(to_blocks(v))
    s = jnp.einsum('brnqhd,brnkhd->brnhqk', qb, kk).astype(jnp.float32) * (Dh ** -0.5)
    qi = jnp.arange(BLOCK)[:, None]
    kj = jnp.arange(2 * BLOCK)[None, :]
    dist = qi + BLOCK - kj
    key_pos = (jnp.arange(nb)[:, None, None] - 1) * BLOCK + kj[None]
    allowed = (dist >= 0)[None] & (dist <= n_back)[None] & (key_pos >= 0)
    s = jnp.where(allowed[None, None, :, None], s, -jnp.inf)
    m = jnp.max(s, axis=-1, keepdims=True)
    p = jnp.exp(s - m)
    l = jnp.sum(p, axis=-1, keepdims=True)
    o = jnp.einsum('brnhqk,brnkhd->brnqhd', p, vv.astype(jnp.float32)) / jnp.swapaxes(l, 3, 4)
    lse = (m + jnp.log(l))[..., 0]
    o = o.reshape(B, r, Lp, H, Dh)[:, :, :L].transpose(0, 2, 1, 3, 4).reshape(B, S, H, Dh)
    lse = lse.transpose(0, 1, 2, 4, 3).reshape(B, r, Lp, H)[:, :, :L].transpose(0, 2, 1, 3).reshape(B, S, H)
    return o, lse


def dilated_mixture_attention(q, k, v):
    B, S = q.shape[:2]
    cos, sin = rope_tables(S)
    q = apply_partial_rope(q, cos, sin)
    k = apply_partial_rope(k, cos, sin)
    outs, lses = [], []
    for g, (window, dilation) in enumerate(DILATED_GROUPS):
        hs = slice(g * HEADS_PER_GROUP, (g + 1) * HEADS_PER_GROUP)
        o, lse = dilated_window_attention(q[:, :, hs], k[:, :, hs], v[:, :, hs], window, dilation)
        outs.append(o)
        lses.append(lse)
    alpha = jax.nn.softmax(jnp.stack(lses, axis=0), axis=0)
    o = jnp.sum(alpha[..., None] * jnp.stack(outs, axis=0), axis=0)
    return o.reshape(B, S, ATTN_OUT_WIDTH).astype(v.dtype)


def multiscale_pool(p, w_pool, pool_scale):
    B, S, C = p.shape
    pf = p.astype(jnp.float32)
    cs0 = jnp.concatenate([jnp.zeros((B, 1, C), jnp.float32), jnp.cumsum(pf, axis=1)], axis=1)
    outs = []
    for gi, win in enumerate(POOL_WINDOWS):
        cs = slice(gi * POOL_GROUP_WIDTH, (gi + 1) * POOL_GROUP_WIDTH)
        c = cs0[..., cs]
        lagged = jnp.pad(c[:, :S - win + 1], ((0, 0), (win - 1, 0), (0, 0)))
        cnt = jnp.minimum(jnp.arange(1, S + 1), win).astype(jnp.float32)[None, :, None]
        outs.append((c[:, 1:] - lagged) / cnt - pf[..., cs])
    d = jnp.stack(outs, axis=2).astype(p.dtype)
    y = jnp.einsum('bsgc,gcd->bsgd', d, w_pool).reshape(B, S, C)
    return y * pool_scale


def setup_inputs(seed: int = 0) -> dict:
    key = jax.random.key(seed)
    ks = jax.random.split(key, 20)
    f32 = jnp.float32

    def nrm(k, shape, fan_in):
        return jax.random.normal(k, shape, f32) * (fan_in ** -0.5)

    def gain(k, shape):
        return 1.0 + 0.02 * jax.random.normal(k, shape, f32)

    L = DEPTH
    return {
        "x": jax.random.normal(ks[0], (BATCH, SEQ, D_MODEL), f32),
        "ffn1_norm": gain(ks[1], (L, D_MODEL)),
        "ffn1_w1": nrm(ks[2], (L, D_MODEL, D_FF), D_MODEL),
        "ffn1_w3": nrm(ks[3], (L, D_MODEL, D_FF), D_MODEL),
        "ffn1_w2": nrm(ks[4], (L, D_FF, D_MODEL), D_FF),
        "mix_norm": gain(ks[5], (L, D_MODEL)),
        "w_in": nrm(ks[6], (L, D_MODEL, IN_WIDTH), D_MODEL),
        "w_branch_attn": nrm(ks[7], (L, ATTN_OUT_WIDTH, D_MODEL), ATTN_OUT_WIDTH),
        "w_branch_pool": nrm(ks[8], (L, POOL_WIDTH, D_MODEL), POOL_WIDTH),
        "pool_w": nrm(ks[9], (L, len(POOL_WINDOWS), POOL_GROUP_WIDTH, POOL_GROUP_WIDTH), POOL_GROUP_WIDTH),
        "pool_scale": gain(ks[10], (L, POOL_WIDTH)),
        "w_out": nrm(ks[11], (L, D_MODEL, D_MODEL), D_MODEL),
        "ffn2_norm": gain(ks[12], (L, D_MODEL)),
        "ffn2_w1": nrm(ks[13], (L, D_MODEL, D_FF), D_MODEL),
        "ffn2_w3": nrm(ks[14], (L, D_MODEL, D_FF), D_MODEL),
        "ffn2_w2": nrm(ks[15], (L, D_FF, D_MODEL), D_FF),
        "final_norm": gain(ks[16], (D_MODEL,)),
    }


def reference(x, ffn1_norm, ffn1_w1, ffn1_w3, ffn1_w2, mix_norm, w_in, w_branch_attn, w_branch_pool,
              pool_w, pool_scale, w_out, ffn2_norm, ffn2_w1, ffn2_w3, ffn2_w2, final_norm):
    B, S, D = x.shape
    h = x
    for l in range(DEPTH):
        h = h + 0.5 * swiglu(rms_norm(h, ffn1_norm[l]), ffn1_w1[l], ffn1_w3[l], ffn1_w2[l])
        u = rms_norm(h, mix_norm[l])
        proj = u @ w_in[l]
        o = 0
        q = proj[..., o:o + ATTN_WIDTH].reshape(B, S, N_ATTN_HEADS, HEAD_DIM); o += ATTN_WIDTH
        k = proj[..., o:o + ATTN_WIDTH].reshape(B, S, N_ATTN_HEADS, HEAD_DIM); o += ATTN_WIDTH
        v = proj[..., o:o + ATTN_WIDTH].reshape(B, S, N_ATTN_HEADS, HEAD_DIM); o += ATTN_WIDTH
        pz = proj[..., o:o + POOL_WIDTH]; o += POOL_WIDTH
        gates = jax.nn.sigmoid(proj[..., o:o + N_BRANCHES * D_MODEL].reshape(B, S, N_BRANCHES, D_MODEL))
        y_attn = dilated_mixture_attention(q, k, v) @ w_branch_attn[l]
        y_pool = multiscale_pool(pz, pool_w[l], pool_scale[l]) @ w_branch_pool[l]
        merged = gates[:, :, 0] * y_attn + gates[:, :, 1] * y_pool
        h = h + merged @ w_out[l]
        h = h + 0.5 * swiglu(rms_norm(h, ffn2_norm[l]), ffn2_w1[l], ffn2_w3[l], ffn2_w2[l])
    return rms_norm(h, final_norm)
```

```python
import numpy as np
from contextlib import ExitStack
import concourse.bass as bass
import concourse.mybir as mybir
from concourse.bass_utils import run_bass_kernel_spmd

F32 = mybir.dt.float32
BF16 = mybir.dt.bfloat16
AF = mybir.ActivationFunctionType
ALU = mybir.AluOpType

D = 1024
T = 2048
NT = 16
DFF = 2816
NFC = 22
FGROUPS = [4, 4, 4, 4, 4, 2]
EPS = 1e-6
NEG = -30000.0
GROUPS = ((128, 1), (512, 4), (2048, 16))
VOFF = (0, 64, 192, 256)
VCOL = (0, 128, 192, 320)

OFF_QKV = 0
OFF_WP = 18432
OFF_PW = OFF_WP + 4096
OFF_WA = OFF_PW + 512
OFF_WB = OFF_WA + 2048
OFF_WO = OFF_WB + 4096
OFF_GT = OFF_WO + 8192
WMIX_COLS = OFF_GT + 8 * 2048
CF_GAIN = 0
CF_CS = 4096
CF_SS = 4352
CF_SM = 4608
CF_COLS = 4608 + 72
SEND_K = {2: (0, 0), 1: (2, 0), 0: (2, 2560)}
SEND_V = {2: (1, 0), 1: (2, 1024), 0: (2, 2816)}
SEND_SIZES = (4096, 6144, 3200)


class _Buf:
    __slots__ = ("name", "arena", "lo", "hi", "w", "r", "over")

    def __init__(self, name, arena=None, lo=0, hi=0):
        self.name, self.arena, self.lo, self.hi = name, arena, lo, hi
        self.w = None
        self.r = {}
        self.over = None


class _Prog:
    COMPUTE = ("pe", "act", "dve", "pool")

    def __init__(self, nc, es):
        self.nc = nc
        self.ops = {e: [] for e in ("pe", "act", "dve", "pool", "sp")}
        self.sem = {e: es.enter_context(nc.semaphore("c_" + e)) for e in self.COMPUTE}
        self.cnt = {e: 0 for e in self.COMPUTE}
        self.seen = {e: {} for e in self.ops}
        self.dsems = {q: [[es.enter_context(nc.semaphore("d_%s%d" % (q, i))), 0] for i in range(n)]
                      for q, n in (("sp", 24), ("pool", 24))}
        self.drr = {"sp": 0, "pool": 0}
        self.ccsems = [[es.enter_context(nc.semaphore("ccsem%d" % i)), 0] for i in range(4)]
        self.bufs = []
        self.semname = {}
        self.targets = {e: set() for e in self.COMPUTE}

    def buf(self, name, arena=None, lo=0, hi=0):
        b = _Buf(name, arena, lo, hi)
        self.bufs.append(b)
        return b

    def _over(self, b):
        if b.over is None:
            if b.arena is None:
                b.over = [b]
            else:
                b.over = [o for o in self.bufs if o.arena == b.arena and o.lo < b.hi and b.lo < o.hi]
        return b.over

    def op(self, eng, fn, reads=(), writes=(), dma=False, cc=None):
        waits = {}

        def need(tok):
            if tok is None:
                return
            if tok[0] == "c":
                if tok[1] == eng and eng == "pe":
                    return
                k, val = ("c", tok[1]), tok[2]
            else:
                k, val = ("d", id(tok[1])), tok[2]
                self.semname[k] = tok[1]
            if self.seen[eng].get(k, 0) >= val:
                return
            if waits.get(k, 0) < val:
                waits[k] = val

        for b in reads:
            for o in self._over(b):
                need(o.w)
        for b in writes:
            for o in self._over(b):
                need(o.w)
                for t in o.r.values():
                    need(t)
        if dma or cc is not None:
            if cc is not None:
                slot = self.ccsems[cc]
                amt = 1
            else:
                lst = self.dsems[eng]
                slot = lst[self.drr[eng] % len(lst)]
                self.drr[eng] += 1
                amt = 16
            if slot[1] > 0:
                need(("d", slot[0], slot[1], eng))
            slot[1] += amt
            tok = ("d", slot[0], slot[1], eng)
            inc = ("d", slot[0], amt)
            rkey = ("d", id(slot[0]))
        else:
            self.cnt[eng] += 1
            tok = ("c", eng, self.cnt[eng])
            inc = ("c", eng, self.cnt[eng])
            rkey = ("c", eng)
        wl = []
        for k, v in waits.items():
            self.seen[eng][k] = v
            wl.append((k, v))
            if k[0] == "c":
                self.targets[k[1]].add(v)
        for b in reads:
            b.r[rkey] = tok
        for b in writes:
            b.w = tok
            b.r = {}
        self.ops[eng].append((fn, wl, inc))
        return tok

    def emit(self, block):
        decos = {"pe": block.tensor, "act": block.scalar, "dve": block.vector,
                 "pool": block.gpsimd, "sp": block.sync}
        for e in self.COMPUTE:
            if self.cnt[e] > 0:
                self.targets[e].add(self.cnt[e])
        rank = {}
        for e in self.COMPUTE:
            rank[e] = {idx: i + 1 for i, idx in enumerate(sorted(self.targets[e]))}
        fin = []
        for e in self.COMPUTE:
            if self.cnt[e] > 0:
                fin.append((self.sem[e], rank[e][self.cnt[e]]))
        for q in self.dsems:
            for s, v in self.dsems[q]:
                if v > 0:
                    fin.append((s, v))
        for s, v in self.ccsems:
            if v > 0:
                fin.append((s, v))
        for name, deco in decos.items():
            def body(eng, name=name):
                for fn, wl, inc in self.ops[name]:
                    for k, v in wl:
                        if k[0] == "c":
                            eng.wait_ge(self.sem[k[1]], rank[k[1]][v])
                        else:
                            eng.wait_ge(self.semname[k], v)
                    ins = fn(eng)
                    if inc[0] == "d":
                        ins.then_inc(inc[1], inc[2])
                    elif inc[2] in rank[inc[1]]:
                        ins.then_inc(self.sem[inc[1]], 1)
                if name == "sp":
                    for s, v in fin:
                        eng.wait_ge(s, v)
            deco(body)


def _build(dbg=None):
    nc = bass.Bass("TRN2", target_bir_lowering=False)
    x_d = nc.dram_tensor("x", [T, D], F32, kind="ExternalInput").ap()
    _skip = bool(dbg) and dbg.startswith("x")
    wf_d = [None, None]
    if not _skip:
        wf_d[0] = nc.dram_tensor("wf1", [128, NFC * 3072], F32, kind="ExternalInput").ap()
    if dbg is None:
        wf_d[1] = nc.dram_tensor("wf2", [128, NFC * 3072], F32, kind="ExternalInput").ap()
    wmix_d = nc.dram_tensor("wmix", [128, WMIX_COLS], F32, kind="ExternalInput").ap()
    cf_d = nc.dram_tensor("cf", [128, CF_COLS], F32, kind="ExternalInput").ap()
    cb_d = nc.dram_tensor("cb", [128, 640], F32, kind="ExternalInput").ap()
    out_d = nc.dram_tensor("out", [T, D], F32, kind="ExternalOutput").ap()
    hsp_d = nc.dram_tensor("hspill", [T, D], F32).ap()
    send_t = [nc.dram_tensor("send%d" % i, [128, n], BF16) for i, n in enumerate(SEND_SIZES)]
    recv_t = [nc.dram_tensor("recv%d" % i, [256, n], BF16) for i, n in enumerate(SEND_SIZES)]
    send_t.append(nc.dram_tensor("send3", [128, 64], F32))
    recv_t.append(nc.dram_tensor("recv3", [256, 64], F32))
    send_d = [t.ap() for t in send_t]
    recv_d = [t.ap() for t in recv_t]
    x_v = x_d.rearrange("(t p) d -> p t d", p=128)
    out_v = out_d.rearrange("(t p) d -> p t d", p=128)
    hsp_v = hsp_d.rearrange("(t p) d -> p t d", p=128)

    with ExitStack() as es:
        def sb(name, shape, dt):
            return es.enter_context(nc.sbuf_tensor(name, shape, dt))

        H = sb("H", [128, 32768], BF16)
        U = sb("U", [128, 8, T], BF16)
        W = sb("W", [128, 24576], BF16)
        X = sb("X", [128, 16384], BF16)
        ident = sb("ident", [128, 128], BF16)
        mask_nh = sb("mask_nh", [128, 256], BF16)
        mask_h = sb("mask_h", [128, 256], BF16)
        cs2 = sb("cs2", [128, NT, 16], F32)
        ss2 = sb("ss2", [128, NT, 16], F32)
        gain_b = sb("gain_b", [128, D], F32)
        smalls = sb("smalls", [128, 72], F32)
        ssq = sb("ssq", [128, NT], F32)
        rstd = sb("rstd", [128, NT], F32)
        xn_tm = [sb("xn_tm%d" % i, [128, D], BF16) for i in range(2)]
        junk = sb("junk", [128, D], BF16)
        qk_tm = [sb("qk_tm%d" % i, [128, 8, 64], BF16) for i in range(2)]
        ropeA = [sb("ropeA%d" % i, [128, 8, 16], F32) for i in range(2)]
        ropeB = [sb("ropeB%d" % i, [128, 8, 16], F32) for i in range(2)]
        ropeX = [sb("ropeX%d" % i, [128, 8, 16], F32) for i in range(2)]
        pT = [sb("pT%d" % i, [128, 256], BF16) for i in range(4)]
        psend = sb("psend", [128, 4, 16], F32)
        phalo = sb("phalo", [128, 4, 16], F32)
        pl16 = sb("pl16", [128, 4, 16], F32)
        pfix = sb("pfix", [128, 32], F32)
        fa = sb("fa", [128, 32], F32)
        fb = sb("fb", [128, 32], F32)
        d16 = sb("d16", [128, 16], BF16)
        banks = [es.enter_context(nc.psum_tensor("bank%d" % i, [128, 512], F32)) for i in range(8)]

        pg = _Prog(nc, es)
        block = es.enter_context(nc.Block())

        def f32v(ap):
            return ap.bitcast(F32)

        h_all = f32v(H[:, :]).rearrange("p (t d) -> p t d", d=D)
        kT_loc = H[:, 0:4096].rearrange("p (c t) -> p c t", c=2)
        v_loc = H[:, 4096:10240].rearrange("p (b c) -> p b c", c=384)
        kT_hal = H[:, 10240:14336].rearrange("p (c t) -> p c t", c=2)
        v_hal = H[:, 14336:20480].rearrange("p (b c) -> p b c", c=384)
        qT = H[:, 20480:24576].rearrange("p (c t) -> p c t", c=2)
        rl_tmp = f32v(H[:, 24576:26624])
        p_sb = f32v(H[:, 0:4128])
        sA = f32v(H[:, 4128:8256])
        sB = f32v(H[:, 8256:12384])
        d_sb = H[:, 12384:14432]
        acc = f32v(X[:, :]).rearrange("p (s t) -> p s t", s=4)
        ypT = X[:, 0:8192].rearrange("p (c t) -> p c t", c=4)
        mergedT = X[:, 8192:12288].rearrange("p (c t) -> p c t", c=8)
        sig = [[f32v(X[:, 12288 + (2 * s + k) * 1024: 12288 + (2 * s + k + 1) * 1024]) for k in range(2)]
               for s in range(2)]
        gT = [X[:, s * 2048:(s + 1) * 2048].rearrange("p (c t) -> p c t", c=4) for s in range(2)]
        s_tmp = [f32v(X[:, 4096 + s * 1024: 4096 + (s + 1) * 1024]) for s in range(2)]
        ostage = [f32v(X[:, 8192 + s * 2048: 8192 + (s + 1) * 2048]) for s in range(2)]
        oT = W[:, 20480:24576].rearrange("p (c t) -> p c t", c=2)
        Wo_v = W[:, 12288:20480].rearrange("p (c n) -> p c n", c=8)
        wp_v = W[:, 0:4096].rearrange("p (c n) -> p c n", c=8)
        pw_v = W[:, 4096:4608].rearrange("p (g n) -> p g n", g=4)
        WA_v = W[:, 4608:6656].rearrange("p (c n) -> p c n", c=2)
        WB_v = W[:, 6656:10752].rearrange("p (c n) -> p c n", c=4)
        gate_v = [W[:, s * 2048:(s + 1) * 2048].rearrange("p (c n) -> p c n", c=8) for s in range(2)]
        qkv_v = [W[:, s * 6144:(s + 1) * 6144].rearrange("p (c n) -> p c n", c=8) for s in range(2)]

        B = pg.buf
        b_h = [B("h%d" % t, "H", t * 4096, (t + 1) * 4096) for t in range(NT)]
        b_kT = B("kT", "H", 0, 8192)
        b_v = B("v", "H", 8192, 20480)
        b_kTh = B("kTh", "H", 20480, 28672)
        b_vh = B("vh", "H", 28672, 40960)
        b_qT = B("qT", "H", 40960, 49152)
        b_rl = B("rl", "H", 49152, 53248)
        b_psb = B("p_sb", "H", 0, 8256)
        b_sA = B("sA", "H", 8256, 16512)
        b_sB = B("sB", "H", 16512, 24768)
        b_dsb = B("d_sb", "H", 24768, 28864)
        b_U = [[B("U%d_%d" % (c, t)) for t in range(NT)] for c in range(2)]
        b_acc = B("acc", "X", 0, 32768)
        b_ypT = [B("ypT%d" % g, "X", g * 4096, (g + 1) * 4096) for g in range(4)]
        b_mrg = B("mergedT", "X", 16384, 24576)
        b_sig = [[B("sig%d%d" % (s, k), "X", 24576 + (2 * s + k) * 2048, 24576 + (2 * s + k + 1) * 2048)
                  for k in range(2)] for s in range(2)]
        b_gT = [B("gT%d" % s, "X", s * 4096, (s + 1) * 4096) for s in range(2)]
        b_st = [B("s_tmp%d" % s, "X", 8192 + s * 2048, 8192 + (s + 1) * 2048) for s in range(2)]
        b_os = [B("ostage%d" % s, "X", 16384 + s * 4096, 16384 + (s + 1) * 4096) for s in range(2)]
        b_wf = [B("wf%d" % s, "W", s * 24576, (s + 1) * 24576) for s in range(2)]
        b_qkv = [B("qkv%d" % s, "W", s * 12288, (s + 1) * 12288) for s in range(2)]
        b_Wo = B("Wo", "W", 24576, 40960)
        b_oT = B("oT", "W", 40960, 49152)
        b_wp = B("wp", "W", 0, 8192)
        b_pw = B("pw", "W", 8192, 9216)
        b_WA = B("WA", "W", 9216, 13312)
        b_WB = B("WB", "W", 13312, 21504)
        b_gate = [B("gate%d" % s, "W", s * 4096, (s + 1) * 4096) for s in range(2)]
        b_ident, b_mnh, b_mh = B("ident"), B("mask_nh"), B("mask_h")
        b_cs, b_ssn, b_gain, b_sm = B("cs2"), B("ss2"), B("gain_b"), B("smalls")
        b_ssq = [B("ssq%d" % i) for i in range(4)]
        b_rstd = [B("rstd%d" % i) for i in range(4)]
        b_junk = B("junk")
        b_xn = [B("xn_tm0"), B("xn_tm1")]
        b_qk = [B("qk_tm0"), B("qk_tm1")]
        b_rA = [B("ropeA0"), B("ropeA1")]
        b_rB = [B("ropeB0"), B("ropeB1")]
        b_rX = [B("ropeX0"), B("ropeX1")]
        b_pT = [B("pT%d" % i) for i in range(4)]
        b_psend, b_phalo, b_pl16 = B("psend"), B("phalo"), B("pl16")
        b_pfix, b_fa, b_fb, b_d16 = B("pfix"), B("fa"), B("fb"), B("d16")
        b_bank = [B("bank%d" % i) for i in range(8)]
        b_hsp = [B("hsp%d" % t) for t in range(NT)]
        b_send = [B("send%d" % i) for i in range(4)]
        b_recv = [B("recv%d" % i) for i in range(4)]
        b_out = [B("out%d" % t) for t in range(NT)]

        OP = pg.op

        OP("sp", lambda e: e.dma_start(out=cs2[:].rearrange("p a b -> p (a b)"), in_=cf_d[:, CF_CS:CF_CS + 256]),
           writes=[b_cs], dma=True)
        OP("sp", lambda e: e.dma_start(out=ss2[:].rearrange("p a b -> p (a b)"), in_=cf_d[:, CF_SS:CF_SS + 256]),
           writes=[b_ssn], dma=True)
        OP("sp", lambda e: e.dma_start(out=smalls[:], in_=cf_d[:, CF_SM:CF_SM + 72]), writes=[b_sm], dma=True)
        OP("pool", lambda e: e.dma_start(out=ident[:], in_=cb_d[:, 0:128]), writes=[b_ident], dma=True)
        OP("pool", lambda e: e.dma_start(out=mask_nh[:], in_=cb_d[:, 128:384]), writes=[b_mnh], dma=True)
        OP("pool", lambda e: e.dma_start(out=mask_h[:], in_=cb_d[:, 384:640]), writes=[b_mh], dma=True)
        pool_scale = smalls[:, 0:4]
        has_prev = smalls[:, 4:5]
        invcnt = smalls[:, 8:72].rearrange("p (g j) -> p g j", g=4)

        for k in range(4):
            OP("sp", lambda e, k=k: e.dma_start(out=h_all[:, 4 * k:4 * k + 4, :], in_=x_v[:, 4 * k:4 * k + 4, :]),
               writes=b_h[4 * k:4 * k + 4], dma=True)

        def load_gain(i):
            OP("sp", lambda e: e.dma_start(out=gain_b[:], in_=cf_d[:, CF_GAIN + i * D: CF_GAIN + (i + 1) * D]),
               writes=[b_gain], dma=True)

        def norm_stats(tb):
            ts_ = slice(4 * tb, 4 * tb + 4)
            OP("dve", lambda e: e.memset(ssq[:, ts_], 0.0), writes=[b_ssq[tb]])
            for t in range(4 * tb, 4 * tb + 4):
                OP("act", lambda e, t=t: e.activation(junk[:], h_all[:, t, :], AF.Square, accum_out=ssq[:, t:t + 1]),
                   reads=[b_h[t]], writes=[b_junk, b_ssq[tb]])
            OP("act", lambda e: e.activation(rstd[:, ts_], ssq[:, ts_], AF.Sqrt, scale=1.0 / D, bias=EPS),
               reads=[b_ssq[tb]], writes=[b_rstd[tb]])
            OP("dve", lambda e: e.reciprocal(rstd[:, ts_], rstd[:, ts_]), reads=[b_rstd[tb]], writes=[b_rstd[tb]])

        def norm_block(tb):
            norm_stats(tb)
            for t in range(4 * tb, 4 * tb + 4):
                s = t % 2
                OP("dve", lambda e, t=t, s=s: e.scalar_tensor_tensor(
                    out=xn_tm[s][:], in0=h_all[:, t, :], scalar=rstd[:, t:t + 1], in1=gain_b[:],
                    op0=ALU.mult, op1=ALU.mult),
                   reads=[b_h[t], b_rstd[tb], b_gain], writes=[b_xn[s]])
                for half in range(2):
                    bk = 2 * s + half

                    def tr(e, s=s, half=half, bk=bk):
                        ins = None
                        for j in range(4):
                            dc = half * 4 + j
                            ins = e.matmul(banks[bk][:, j * 128:(j + 1) * 128], xn_tm[s][:, dc * 128:(dc + 1) * 128],
                                           ident[:], start=True, stop=True)
                        return ins
                    OP("pe", tr, reads=[b_xn[s], b_ident], writes=[b_bank[bk]])
                    src = banks[bk][:, :].rearrange("p (c t) -> p c t", c=4)
                    dst = U[:, half * 4:half * 4 + 4, t * 128:(t + 1) * 128]
                    if half == 0:
                        OP("act", lambda e, src=src, dst=dst: e.activation(dst, src, AF.Copy),
                           reads=[b_bank[bk]], writes=[b_U[half][t]])
                    else:
                        OP("dve", lambda e, src=src, dst=dst: e.tensor_copy(dst, src),
                           reads=[b_bank[bk]], writes=[b_U[half][t]])

        def final_block(tb):
            norm_stats(tb)
            for t in range(4 * tb, 4 * tb + 4):
                s = t % 2
                OP("dve", lambda e, t=t, s=s: e.scalar_tensor_tensor(
                    out=ostage[s][:], in0=h_all[:, t, :], scalar=rstd[:, t:t + 1], in1=gain_b[:],
                    op0=ALU.mult, op1=ALU.mult),
                   reads=[b_h[t], b_rstd[tb], b_gain], writes=[b_os[s]])
                OP("sp", lambda e, t=t, s=s: e.dma_start(out=out_v[:, t, :], in_=ostage[s][:]),
                   reads=[b_os[s]], writes=[b_out[t]], dma=True)

        def U_reads(tb):
            return [b_U[c][t] for c in range(2) for t in range(4 * tb, 4 * tb + 4)]

        U_all = [b_U[c][t] for c in range(2) for t in range(NT)]

        wcount = [0]

        def ffn(wd, pre=None, post=None):
            c0 = 0
            ybank = [0]
            for gi, G in enumerate(FGROUPS):
                slot = wcount[0] % 2
                wcount[0] += 1
                base = slot * 12288
                n_el = G * 3072
                OP("pool", lambda e, base=base, c0=c0, n_el=n_el: e.dma_start(
                    out=W[:, base:base + n_el], in_=wd[:, c0 * 3072:c0 * 3072 + n_el]),
                   writes=[b_wf[slot]], dma=True)
                w1 = W[:, base:base + G * 1024].rearrange("p (c g j) -> p c g j", c=8, g=G)
                w3 = W[:, base + G * 1024:base + 2 * G * 1024].rearrange("p (c g j) -> p c g j", c=8, g=G)
                w2 = W[:, base + 2 * G * 1024:base + 3 * G * 1024].rearrange("p (g n) -> p g n", g=G)
                for tb in range(4):
                    gs = tb % 2
                    if gi == 0 and pre is not None:
                        pre(tb)
                    for fcl in range(G):
                        ub = (fcl % 2) * 2
                        for which, wv in ((0, w1), (1, w3)):
                            def up(e, wv=wv, fcl=fcl, tb=tb, bk=ub + which):
                                ins = None
                                for dc in range(8):
                                    ins = e.matmul(banks[bk][:, :], wv[:, dc, fcl, :], U[:, dc, tb * 512:(tb + 1) * 512],
                                                   start=(dc == 0), stop=(dc == 7))
                                return ins
                            OP("pe", up, reads=[b_wf[slot]] + U_reads(tb), writes=[b_bank[ub + which]])
                        st = fcl % 2
                        OP("act", lambda e, st=st, ub=ub: e.activation(s_tmp[st][:], banks[ub][:, :], AF.Silu),
                           reads=[b_bank[ub]], writes=[b_st[st]])
                        OP("dve", lambda e, st=st, ub=ub, gs=gs, fcl=fcl: e.tensor_tensor(
                            gT[gs][:, fcl, :], s_tmp[st][:], banks[ub + 1][:, :], ALU.mult),
                           reads=[b_st[st], b_bank[ub + 1]], writes=[b_gT[gs]])
                    for ts in range(4):
                        tt = tb * 4 + ts
                        for dh in range(2):
                            bk = 4 + (ybank[0] % 4)
                            ybank[0] += 1

                            def down(e, gs=gs, ts=ts, dh=dh, bk=bk, G=G, w2=w2):
                                ins = None
                                for fcl in range(G):
                                    ins = e.matmul(banks[bk][:, :], gT[gs][:, fcl, ts * 128:(ts + 1) * 128],
                                                   w2[:, fcl, dh * 512:(dh + 1) * 512],
                                                   start=(fcl == 0), stop=(fcl == G - 1))
                                return ins
                            OP("pe", down, reads=[b_gT[gs], b_wf[slot]], writes=[b_bank[bk]])
                            hv = h_all[:, tt, dh * 512:(dh + 1) * 512]
                            OP("dve", lambda e, hv=hv, bk=bk: e.scalar_tensor_tensor(
                                out=hv, in0=banks[bk][:, :], scalar=0.5, in1=hv, op0=ALU.mult, op1=ALU.add),
                               reads=[b_bank[bk], b_h[tt]], writes=[b_h[tt]])
                    if gi == len(FGROUPS) - 1 and post is not None:
                        post(tb)
                c0 += G

        skipffn = bool(dbg) and dbg.startswith("x")
        stage = dbg[1:] if skipffn else dbg

        def early_exit():
            for t in range(NT):
                OP("sp", lambda e, t=t: e.dma_start(out=out_v[:, t, :], in_=h_all[:, t, :]),
                   reads=[b_h[t]], writes=[b_out[t]], dma=True)
            pg.emit(block)
            return nc

        def spill(tb):
            OP("sp", lambda e, k=tb: e.dma_start(out=hsp_v[:, 4 * k:4 * k + 4, :], in_=h_all[:, 4 * k:4 * k + 4, :]),
               reads=b_h[4 * tb:4 * tb + 4], writes=b_hsp[4 * tb:4 * tb + 4], dma=True)

        def post_ffn1(tb):
            if tb == 0:
                load_gain(1)
            norm_block(tb)
            spill(tb)

        load_gain(0)
        if not skipffn:
            ffn(wf_d[0], pre=norm_block, post=None if stage == "ffn1" else post_ffn1)
        else:
            for tb in range(4):
                norm_block(tb)
            for tb in range(4):
                post_ffn1(tb)
        if stage == "ffn1":
            for t in range(NT):
                OP("sp", lambda e, t=t: e.dma_start(out=out_v[:, t, :], in_=h_all[:, t, :]),
                   reads=[b_h[t]], writes=[b_out[t]], dma=True)
            pg.emit(block)
            return nc


        qkvslot = [0]
        rr = {"qk": 0, "bk01": 0, "bk23": 0, "bk45": 0, "pT": 0, "po": 0}

        def load_qkv(g):
            slot = qkvslot[0] % 2
            qkvslot[0] += 1
            OP("pool", lambda e, slot=slot, g=g: e.dma_start(
                out=W[:, slot * 6144:(slot + 1) * 6144], in_=wmix_d[:, OFF_QKV + g * 6144: OFF_QKV + (g + 1) * 6144]),
               writes=[b_qkv[slot]], dma=True)
            return slot

        def gpos(r, tile):
            return tile * (128 // r)

        def proj_qk(g, slot, tiles):
            r = GROUPS[g][1]
            wv = qkv_v[slot]
            pend = None
            for t in tiles:
                ctx = proj_qk_stage1(r, wv, slot, t)
                if pend is not None:
                    proj_qk_stage2(r, *pend)
                pend = ctx
            proj_qk_stage2(r, *pend)

        def proj_qk_stage1(r, wv, slot, t):
            if True:
                bk = rr["bk01"] % 2
                rr["bk01"] += 1

                def mm(e, t=t, bk=bk, wv=wv):
                    ins = None
                    for dc in range(8):
                        ins = e.matmul(banks[bk][:, :], U[:, dc, t * 128:(t + 1) * 128], wv[:, dc, 0:512],
                                       start=(dc == 0), stop=(dc == 7))
                    return ins
                OP("pe", mm, reads=[b_qkv[slot], b_U[0][t], b_U[1][t]], writes=[b_bank[bk]])
                s = rr["qk"] % 2
                rr["qk"] += 1
                ps = banks[bk][:, :].rearrange("p (h d) -> p h d", h=8)
                csb = cs2[:, t, :].unsqueeze(1).broadcast_to([128, 8, 16])
                ssb = ss2[:, t, :].unsqueeze(1).broadcast_to([128, 8, 16])
                OP("act", lambda e, ps=ps, s=s: e.activation(ropeX[s][:], ps[:, :, 0:16], AF.Copy),
                   reads=[b_bank[bk]], writes=[b_rX[s]])
                OP("act", lambda e, ps=ps, s=s: e.activation(qk_tm[s][:, :, 16:64], ps[:, :, 16:64], AF.Copy),
                   reads=[b_bank[bk]], writes=[b_qk[s]])
                OP("dve", lambda e, s=s, csb=csb: e.tensor_tensor(ropeA[s][:], ropeX[s][:], csb, ALU.mult),
                   reads=[b_rX[s], b_cs], writes=[b_rA[s]])
                OP("dve", lambda e, s=s, ssb=ssb: e.tensor_tensor(ropeB[s][:, :, 0:8], ropeX[s][:, :, 8:16], ssb[:, :, 0:8], ALU.mult),
                   reads=[b_rX[s], b_ssn], writes=[b_rB[s]])
                OP("dve", lambda e, s=s, ssb=ssb: e.tensor_tensor(ropeB[s][:, :, 8:16], ropeX[s][:, :, 0:8], ssb[:, :, 8:16], ALU.mult),
                   reads=[b_rX[s], b_ssn], writes=[b_rB[s]])
                OP("dve", lambda e, s=s: e.tensor_tensor(qk_tm[s][:, :, 0:16], ropeA[s][:], ropeB[s][:], ALU.add),
                   reads=[b_rA[s], b_rB[s]], writes=[b_qk[s]])
                return (t, s)

        def proj_qk_stage2(r, t, s):
            if True:
                tk = 4 + (rr["bk45"] % 2)
                rr["bk45"] += 1
                flat = qk_tm[s][:].rearrange("p h d -> p (h d)")

                def tr(e, flat=flat, tk=tk):
                    ins = None
                    for j in range(4):
                        ins = e.matmul(banks[tk][:, j * 128:(j + 1) * 128], flat[:, j * 128:(j + 1) * 128], ident[:],
                                       start=True, stop=True)
                    return ins
                OP("pe", tr, reads=[b_qk[s], b_ident], writes=[b_bank[tk]])
                nper = 128 // r
                i0 = t * nper
                for which, dstT, dbuf in ((0, qT, b_qT), (1, kT_loc, b_kT)):
                    src = banks[tk][:, which * 256:(which + 1) * 256].rearrange("p (ch i c) -> p ch c i", ch=2, c=r)
                    dst = dstT[:, :, :].rearrange("p ch (c i) -> p ch c i", c=r)[:, :, :, i0:i0 + nper]
                    OP("act", lambda e, src=src, dst=dst: e.activation(dst, src, AF.Copy),
                       reads=[b_bank[tk]], writes=[dbuf])

        def proj_v(g, slot, blocks):
            r = GROUPS[g][1]
            nb = NT // r
            wv = qkv_v[slot]
            for b in blocks:
                c, n = b // nb, b % nb
                start = 128 * n * r + c
                bk = 2 + (rr["bk23"] % 2)
                rr["bk23"] += 1

                def mm(e, start=start, r=r, bk=bk, wv=wv):
                    ins = None
                    for dc in range(8):
                        ins = e.matmul(banks[bk][:, 0:256], U[:, dc, start:start + 127 * r + 1:r], wv[:, dc, 512:768],
                                       start=(dc == 0), stop=(dc == 7))
                    return ins
                OP("pe", mm, reads=[b_qkv[slot]] + U_all, writes=[b_bank[bk]])
                src01 = banks[bk][:, 0:128].rearrange("p (h d) -> p h d", h=2)
                dst01 = v_loc[:, b, 0:256].rearrange("p (h d) -> p h d", h=2)[:, :, 0:64]
                d2 = v_loc[:, b, 192:256]
                d3 = v_loc[:, b, 320:384]
                if b % 2 == 0:
                    OP("act", lambda e, src01=src01, dst01=dst01: e.activation(dst01, src01, AF.Copy),
                       reads=[b_bank[bk]], writes=[b_v])
                    OP("act", lambda e, bk=bk, d2=d2: e.activation(d2, banks[bk][:, 128:192], AF.Copy),
                       reads=[b_bank[bk]], writes=[b_v])
                    OP("act", lambda e, bk=bk, d3=d3: e.activation(d3, banks[bk][:, 192:256], AF.Copy),
                       reads=[b_bank[bk]], writes=[b_v])
                else:
                    OP("dve", lambda e, src01=src01, dst01=dst01: e.tensor_copy(dst01, src01),
                       reads=[b_bank[bk]], writes=[b_v])
                    OP("dve", lambda e, bk=bk, d2=d2: e.tensor_copy(d2, banks[bk][:, 128:192]),
                       reads=[b_bank[bk]], writes=[b_v])
                    OP("dve", lambda e, bk=bk, d3=d3: e.tensor_copy(d3, banks[bk][:, 192:256]),
                       reads=[b_bank[bk]], writes=[b_v])

        def set_ones(vt, bufv):
            OP("dve", lambda e: e.memset(vt[:, :, 64:128], 1.0), writes=[bufv])
            OP("dve", lambda e: e.memset(vt[:, :, 256:320], 1.0), writes=[bufv])

        def allgather(i):
            OP("pool", lambda e, i=i: e.collective_compute(
                "AllGather", ALU.bypass, replica_groups=[[0, 1], [2, 3], [4, 5], [6, 7]],
                ins=[send_t[i].ap().opt()], outs=[recv_t[i].ap().opt()]),
               reads=[b_send[i]], writes=[b_recv[i]], cc=i)

        for g in (2, 1, 0):
            r = GROUPS[g][1]
            nb = NT // r
            slot = load_qkv(g)
            tiles = list(range(NT)) if g == 2 else (list(range(12, 16)) if g == 1 else [15])
            blocks = [c * nb + nb - 1 for c in range(r)]
            if g == 2 and stage == "A0":
                return early_exit()
            if g == 2:
                set_ones(v_loc, b_v)
            if g == 2 and stage == "A1":
                return early_exit()
            proj_qk(g, slot, tiles if not (g == 2 and stage == "A2s") else tiles[:1])
            if g == 2 and stage in ("A2", "A2s"):
                return early_exit()
            proj_v(g, slot, blocks)
            if g == 2 and stage == "A3":
                return early_exit()
            (kt_i, ko), (vt_i, vo) = SEND_K[g], SEND_V[g]
            ksrc = kT_loc[:, :, :].rearrange("p ch (c i) -> p ch c i", c=r)[:, :, :, (nb - 1) * 128:nb * 128]
            kdst = send_d[kt_i][:, ko:ko + 256 * r].rearrange("p (ch c i) -> p ch c i", ch=2, c=r)
            for ch in range(2):
                OP("sp", lambda e, ksrc=ksrc, kdst=kdst, ch=ch: e.dma_start(out=kdst[:, ch], in_=ksrc[:, ch]),
                   reads=[b_kT], writes=[b_send[kt_i]], dma=True)
            vsrc = v_loc[:, :, :].rearrange("p (c n) k -> p c n k", c=r)[:, :, nb - 1, :]
            vdst = send_d[vt_i][:, vo:vo + 384 * r].rearrange("p (c k) -> p c k", c=r)
            OP("sp", lambda e, vsrc=vsrc, vdst=vdst: e.dma_start(out=vdst, in_=vsrc),
               reads=[b_v], writes=[b_send[vt_i]], dma=True)
            if g == 2 and stage == "A4":
                return early_exit()
            if g == 2:
                for i in (0, 1):
                    allgather(i)
            if g == 2 and stage == "A5":
                return early_exit()

        OP("pool", lambda e: e.dma_start(out=W[:, 0:4096], in_=wmix_d[:, OFF_WP:OFF_WP + 4096]),
           writes=[b_wp], dma=True)

        def p_proj(gi, tb, bk):
            def mm(e):
                ins = None
                for dc in range(8):
                    ins = e.matmul(banks[bk][:, :], wp_v[:, dc, gi * 128:(gi + 1) * 128], U[:, dc, tb * 512:(tb + 1) * 512],
                                   start=(dc == 0), stop=(dc == 7))
                return ins
            OP("pe", mm, reads=[b_wp] + U_reads(tb), writes=[b_bank[bk]])

        for gi in range(4):
            bk = gi % 2
            p_proj(gi, 3, bk)
            OP("act", lambda e, gi=gi, bk=bk: e.activation(psend[:, gi, :], banks[bk][:, 496:512], AF.Copy),
               reads=[b_bank[bk]], writes=[b_psend])
        allgather(2)
        OP("sp", lambda e: e.dma_start(out=send_d[3], in_=psend[:].rearrange("p g j -> p (g j)")),
           reads=[b_psend], writes=[b_send[3]], dma=True)
        allgather(3)
        OP("pool", lambda e: e.dma_start(out=W[:, 12288:20480], in_=wmix_d[:, OFF_WO:OFF_WO + 8192]),
           writes=[b_Wo], dma=True)

        if stage == "A":
            return early_exit()

        first_group = [True]

        def attention(g):
            r = GROUPS[g][1]
            nb = NT // r
            slot = load_qkv(g)
            proj_qk(g, slot, list(range(NT)))
            proj_v(g, slot, list(range(NT)))
            order = [(c, n) for c in range(r) for n in range(1, nb)] + [(c, 0) for c in range(r)]
            halo_loaded = [False]
            pend = [None]

            def emit_pv(ctx):
                (vprev, vcur, pi, pb, h, rd, fin) = ctx

                def pvmm(e, vprev=vprev, vcur=vcur, pi=pi, pb=pb, h=h):
                    e.matmul(banks[pb][:, h * 128:(h + 1) * 128], vprev, pT[pi][:, 0:128], start=True, stop=False)
                    return e.matmul(banks[pb][:, h * 128:(h + 1) * 128], vcur, pT[pi][:, 128:256], start=False, stop=True)
                OP("pe", pvmm, reads=rd + [b_pT[pi]], writes=[b_bank[pb]])
                if fin is not None:
                    (c, n) = fin
                    st = 128 * n * r + c
                    dst = acc[:, :, st:st + 127 * r + 1:r]
                    src_ = banks[pb][:, :].rearrange("p (h t) -> p h t", h=4)
                    if first_group[0]:
                        OP("dve", lambda e, dst=dst, src_=src_: e.tensor_copy(dst, src_), reads=[b_bank[pb]], writes=[b_acc])
                    else:
                        OP("dve", lambda e, dst=dst, src_=src_: e.tensor_tensor(dst, src_, dst, ALU.add),
                           reads=[b_bank[pb], b_acc], writes=[b_acc])

            for (c, n) in order:
                b = c * nb + n
                if n == 0 and not halo_loaded[0]:
                    (kt_i, ko), (vt_i, vo) = SEND_K[g], SEND_V[g]
                    for ch in range(2):
                        OP("sp", lambda e, ch=ch, ko=ko, r=r, kt_i=kt_i: e.dma_start(
                            out=kT_hal[:, ch, 0:128 * r],
                            in_=recv_d[kt_i][0:128, ko + ch * 128 * r: ko + (ch + 1) * 128 * r]),
                           reads=[b_recv[kt_i]], writes=[b_kTh], dma=True)
                    OP("sp", lambda e, vo=vo, r=r, vt_i=vt_i: e.dma_start(
                        out=v_hal[:, 0:r, :],
                        in_=recv_d[vt_i][0:128, vo:vo + 384 * r].rearrange("p (c k) -> p c k", c=r)),
                       reads=[b_recv[vt_i]], writes=[b_vh], dma=True)
                    halo_loaded[0] = True
                pb = 6 + (rr["po"] % 2)
                rr["po"] += 1
                for h in range(4):
                    ch, pr = h // 2, (h % 2) * 64
                    sbk = 2 + (rr["bk23"] % 2)
                    rr["bk23"] += 1
                    if n > 0:
                        kprev = kT_loc[pr:pr + 64, ch, (b - 1) * 128:b * 128]
                        vprev = v_loc[:, b - 1, VOFF[h]:VOFF[h] + 128]
                        mk, mkb = mask_nh, b_mnh
                        rd = [b_kT, b_v]
                    else:
                        kprev = kT_hal[pr:pr + 64, ch, c * 128:(c + 1) * 128]
                        vprev = v_hal[:, c, VOFF[h]:VOFF[h] + 128]
                        mk, mkb = mask_h, b_mh
                        rd = [b_kT, b_v, b_kTh, b_vh]
                    kcur = kT_loc[pr:pr + 64, ch, b * 128:(b + 1) * 128]
                    vcur = v_loc[:, b, VOFF[h]:VOFF[h] + 128]
                    qv = qT[pr:pr + 64, ch, b * 128:(b + 1) * 128]

                    def smm(e, kprev=kprev, kcur=kcur, qv=qv, mk=mk, sbk=sbk):
                        e.matmul(banks[sbk][:, 0:128], kprev, qv, start=True, stop=False)
                        e.matmul(banks[sbk][:, 128:256], kcur, qv, start=False, stop=False)
                        return e.matmul(banks[sbk][:, 0:256], ident[:], mk[:], start=False, stop=True)
                    OP("pe", smm, reads=rd + [b_qT, b_ident, mkb], writes=[b_bank[sbk]])
                    pi = rr["pT"] % 4
                    rr["pT"] += 1
                    OP("act", lambda e, pi=pi, sbk=sbk: e.activation(pT[pi][:], banks[sbk][:, 0:256], AF.Exp, scale=0.125),
                       reads=[b_bank[sbk]], writes=[b_pT[pi]])
                    if pend[0] is not None:
                        emit_pv(pend[0])
                    pend[0] = (vprev, vcur, pi, pb, h, rd, (c, n) if h == 3 else None)
            emit_pv(pend[0])
            first_group[0] = False

        set_ones(v_hal, b_vh)
        for g in (0, 1, 2):
            if g > 0:
                pass
            attention(g)
            if g == 0:
                pass

        if stage == "B":
            return early_exit()

        for j in range(4):
            for half in range(2):
                cs_ = slice(half * 1024, (half + 1) * 1024)
                if j % 2 == 0:
                    lrow, orow = slice(64, 128), slice(0, 64)
                else:
                    lrow, orow = slice(0, 64), slice(64, 128)
                OP("dve", lambda e, j=j, cs_=cs_, lrow=lrow: e.reciprocal(rl_tmp[lrow, :], acc[lrow, j, cs_]),
                   reads=[b_acc], writes=[b_rl])
                OP("dve", lambda e, lrow=lrow, orow=orow: e.tensor_copy(rl_tmp[orow, :], rl_tmp[lrow, :]),
                   reads=[b_rl], writes=[b_rl])
                OP("dve", lambda e, j=j, cs_=cs_, orow=orow: e.tensor_tensor(
                    oT[orow, j // 2, cs_], acc[orow, j, cs_], rl_tmp[orow, :], ALU.mult),
                   reads=[b_acc, b_rl], writes=[b_oT])

        if stage == "N":
            return early_exit()

        OP("pool", lambda e: e.dma_start(out=W[:, 0:4096], in_=wmix_d[:, OFF_WP:OFF_WP + 4096]),
           writes=[b_wp], dma=True)
        OP("pool", lambda e: e.dma_start(out=W[:, 4096:4608], in_=wmix_d[:, OFF_PW:OFF_PW + 512]),
           writes=[b_pw], dma=True)
        OP("pool", lambda e: e.dma_start(out=W[:, 4608:6656], in_=wmix_d[:, OFF_WA:OFF_WA + 2048]),
           writes=[b_WA], dma=True)
        OP("pool", lambda e: e.dma_start(out=W[:, 6656:10752], in_=wmix_d[:, OFF_WB:OFF_WB + 4096]),
           writes=[b_WB], dma=True)
        OP("sp", lambda e: e.dma_start(out=phalo[:].rearrange("p g j -> p (g j)"), in_=recv_d[3][0:128, :]),
           reads=[b_recv[3]], writes=[b_phalo], dma=True)
        OP("dve", lambda e: e.tensor_scalar(phalo[:], phalo[:], has_prev, None, ALU.mult),
           reads=[b_phalo, b_sm], writes=[b_phalo])
        OP("dve", lambda e: e.memset(p_sb[:, 0:16], 0.0), writes=[b_psb])
        OP("dve", lambda e: e.memset(sA[:, 0:16], 0.0), writes=[b_sA])
        OP("dve", lambda e: e.memset(sB[:, 0:16], 0.0), writes=[b_sB])
        for gi in range(4):
            win = 2 ** (gi + 1)
            for tb in range(4):
                bk = tb % 4
                p_proj(gi, tb, bk)
                OP("act", lambda e, tb=tb, bk=bk: e.activation(p_sb[:, 16 + tb * 512:16 + (tb + 1) * 512], banks[bk][:, :], AF.Copy),
                   reads=[b_bank[bk]], writes=[b_psb])
            OP("act", lambda e, gi=gi: e.activation(pl16[:, gi, :], p_sb[:, 16:32], AF.Copy),
               reads=[b_psb], writes=[b_pl16])
            cur, cb_ = p_sb, b_psb
            bufs2 = [(sA, b_sA), (sB, b_sB)]
            for k in range(gi + 1):
                sh = 2 ** k
                nxt, nb_ = bufs2[k % 2]
                OP("dve", lambda e, cur=cur, nxt=nxt, sh=sh: e.tensor_tensor(
                    nxt[:, 16:16 + T], cur[:, 16:16 + T], cur[:, 16 - sh:16 - sh + T], ALU.add),
                   reads=[cb_], writes=[nb_])
                cur, cb_ = nxt, nb_
            OP("dve", lambda e, cur=cur, win=win: e.scalar_tensor_tensor(
                out=d_sb[:, :], in0=cur[:, 16:16 + T], scalar=1.0 / win, in1=p_sb[:, 16:16 + T],
                op0=ALU.mult, op1=ALU.subtract),
               reads=[cb_, b_psb], writes=[b_dsb])
            for tb in range(4):
                bk = 4 + tb % 4
                OP("pe", lambda e, gi=gi, tb=tb, bk=bk: e.matmul(
                    banks[bk][:, :], pw_v[:, gi, :], d_sb[:, tb * 512:(tb + 1) * 512], start=True, stop=True),
                   reads=[b_pw, b_dsb], writes=[b_bank[bk]])
                OP("act", lambda e, gi=gi, tb=tb, bk=bk: e.activation(
                    ypT[:, gi, tb * 512:(tb + 1) * 512], banks[bk][:, :], AF.Copy, scale=pool_scale[:, gi:gi + 1]),
                   reads=[b_bank[bk], b_sm], writes=[b_ypT[gi]])
            OP("dve", lambda e, gi=gi: e.tensor_copy(pfix[:, 0:16], phalo[:, gi, :]), reads=[b_phalo], writes=[b_pfix])
            OP("dve", lambda e, gi=gi: e.tensor_copy(pfix[:, 16:32], pl16[:, gi, :]), reads=[b_pl16], writes=[b_pfix])
            curf, cfb = pfix, b_pfix
            bufs3 = [(fa, b_fa), (fb, b_fb)]
            lo = 0
            for k in range(gi + 1):
                sh = 2 ** k
                lo += sh
                nxt, nb_ = bufs3[k % 2]
                OP("dve", lambda e, curf=curf, nxt=nxt, sh=sh, lo=lo: e.tensor_tensor(
                    nxt[:, lo:32], curf[:, lo:32], curf[:, lo - sh:32 - sh], ALU.add),
                   reads=[cfb], writes=[nb_])
                curf, cfb = nxt, nb_
            nxt, nb_ = bufs3[(gi + 1) % 2]
            OP("dve", lambda e, curf=curf, nxt=nxt, gi=gi: e.tensor_tensor(
                nxt[:, 16:32], curf[:, 16:32], invcnt[:, gi, :], ALU.mult),
               reads=[cfb, b_sm], writes=[nb_])
            OP("dve", lambda e, nxt=nxt: e.tensor_tensor(d16[:], nxt[:, 16:32], pfix[:, 16:32], ALU.subtract),
               reads=[nb_, b_pfix], writes=[b_d16])
            OP("pe", lambda e, gi=gi: e.matmul(banks[3][:, 0:16], pw_v[:, gi, :], d16[:], start=True, stop=True),
               reads=[b_pw, b_d16], writes=[b_bank[3]])
            OP("act", lambda e, gi=gi: e.activation(ypT[:, gi, 0:16], banks[3][:, 0:16], AF.Copy,
                                                   scale=pool_scale[:, gi:gi + 1]),
               reads=[b_bank[3], b_sm], writes=[b_ypT[gi]])

        if stage == "P":
            return early_exit()

        gslot = [0]
        obank = [0]
        load_gain(2)
        for tb in range(4):
            OP("sp", lambda e, tb=tb: e.dma_start(out=h_all[:, 4 * tb:4 * tb + 4, :], in_=hsp_v[:, 4 * tb:4 * tb + 4, :]),
               reads=b_hsp[4 * tb:4 * tb + 4], writes=b_h[4 * tb:4 * tb + 4], dma=True)
            for dco in range(8):
                gs = gslot[0] % 2
                gslot[0] += 1
                OP("pool", lambda e, gs=gs, dco=dco: e.dma_start(
                    out=W[:, gs * 2048:(gs + 1) * 2048], in_=wmix_d[:, OFF_GT + dco * 2048: OFF_GT + (dco + 1) * 2048]),
                   writes=[b_gate[gs]], dma=True)
                gb = gs * 2
                yb = 4 + gs * 2
                for k in range(2):
                    def gmm(e, gs=gs, k=k, tb=tb, bk=gb + k):
                        ins = None
                        for dc in range(8):
                            ins = e.matmul(banks[bk][:, :], gate_v[gs][:, dc, k * 128:(k + 1) * 128],
                                           U[:, dc, tb * 512:(tb + 1) * 512], start=(dc == 0), stop=(dc == 7))
                        return ins
                    OP("pe", gmm, reads=[b_gate[gs]] + U_reads(tb), writes=[b_bank[gb + k]])

                def yamm(e, dco=dco, tb=tb, bk=yb):
                    ins = None
                    for ch in range(2):
                        ins = e.matmul(banks[bk][:, :], WA_v[:, ch, dco * 128:(dco + 1) * 128],
                                       oT[:, ch, tb * 512:(tb + 1) * 512], start=(ch == 0), stop=(ch == 1))
                    return ins
                OP("pe", yamm, reads=[b_WA, b_oT], writes=[b_bank[yb]])

                def ybmm(e, dco=dco, tb=tb, bk=yb + 1):
                    ins = None
                    for ch in range(4):
                        ins = e.matmul(banks[bk][:, :], WB_v[:, ch, dco * 128:(dco + 1) * 128],
                                       ypT[:, ch, tb * 512:(tb + 1) * 512], start=(ch == 0), stop=(ch == 3))
                    return ins
                OP("pe", ybmm, reads=[b_WB] + b_ypT, writes=[b_bank[yb + 1]])
                for k in range(2):
                    OP("act", lambda e, gs=gs, k=k, bk=gb + k: e.activation(sig[gs][k][:], banks[bk][:, :], AF.Sigmoid),
                       reads=[b_bank[gb + k]], writes=[b_sig[gs][k]])
                    OP("dve", lambda e, gs=gs, k=k, bk=yb + k: e.tensor_tensor(
                        sig[gs][k][:], sig[gs][k][:], banks[bk][:, :], ALU.mult),
                       reads=[b_sig[gs][k], b_bank[yb + k]], writes=[b_sig[gs][k]])
                OP("dve", lambda e, gs=gs, dco=dco: e.tensor_tensor(
                    mergedT[:, dco, :], sig[gs][0][:], sig[gs][1][:], ALU.add),
                   reads=[b_sig[gs][0], b_sig[gs][1]], writes=[b_mrg])
            for ts in range(4):
                tt = tb * 4 + ts
                for dh in range(2):
                    bk = 4 + (obank[0] % 4)
                    obank[0] += 1

                    def omm(e, ts=ts, dh=dh, bk=bk):
                        ins = None
                        for dco in range(8):
                            ins = e.matmul(banks[bk][:, :], mergedT[:, dco, ts * 128:(ts + 1) * 128],
                                           Wo_v[:, dco, dh * 512:(dh + 1) * 512], start=(dco == 0), stop=(dco == 7))
                        return ins
                    OP("pe", omm, reads=[b_mrg, b_Wo], writes=[b_bank[bk]])
                    hv = h_all[:, tt, dh * 512:(dh + 1) * 512]
                    OP("dve", lambda e, hv=hv, bk=bk: e.tensor_tensor(hv, banks[bk][:, :], hv, ALU.add),
                       reads=[b_bank[bk], b_h[tt]], writes=[b_h[tt]])
            if stage != "mix":
                norm_block(tb)

        if stage == "mix":
            for t in range(NT):
                OP("sp", lambda e, t=t: e.dma_start(out=out_v[:, t, :], in_=h_all[:, t, :]),
                   reads=[b_h[t]], writes=[b_out[t]], dma=True)
            pg.emit(block)
            return nc

        def post_ffn2(tb):
            if tb == 0:
                load_gain(3)
            final_block(tb)

        ffn(wf_d[1], pre=None, post=post_ffn2)
        pg.emit(block)
    return nc


def _ffn_layout(w1, w3, w2):
    w1r = w1.reshape(8, 128, NFC, 128)
    w3r = w3.reshape(8, 128, NFC, 128)
    w2r = w2.reshape(NFC, 128, D)
    parts = []
    c0 = 0
    for G in FGROUPS:
        parts.append(w1r[:, :, c0:c0 + G, :].transpose(1, 0, 2, 3).reshape(128, -1))
        parts.append(w3r[:, :, c0:c0 + G, :].transpose(1, 0, 2, 3).reshape(128, -1))
        parts.append(w2r[c0:c0 + G].transpose(1, 0, 2).reshape(128, -1))
        c0 += G
    return np.ascontiguousarray(np.concatenate(parts, axis=1), dtype=np.float32)


def _mix_layout(w_in, w_a, w_b, pool_w, w_o):
    wr = w_in.reshape(8, 128, -1)
    parts = []
    for g in range(3):
        cols = np.concatenate([np.arange(g * 256, (g + 1) * 256), 768 + np.arange(g * 256, (g + 1) * 256),
                               1536 + np.arange(g * 256, (g + 1) * 256)])
        parts.append(wr[:, :, cols].transpose(1, 0, 2).reshape(128, -1))
    parts.append(wr[:, :, 2304:2816].transpose(1, 0, 2).reshape(128, -1))
    parts.append(pool_w.transpose(1, 0, 2).reshape(128, -1))
    parts.append(w_a.reshape(2, 128, D).transpose(1, 0, 2).reshape(128, -1))
    parts.append(w_b.reshape(4, 128, D).transpose(1, 0, 2).reshape(128, -1))
    parts.append(w_o.reshape(8, 128, D).transpose(1, 0, 2).reshape(128, -1))
    for dco in range(8):
        cols = np.concatenate([2816 + np.arange(dco * 128, (dco + 1) * 128), 3840 + np.arange(dco * 128, (dco + 1) * 128)])
        parts.append(wr[:, :, cols].transpose(1, 0, 2).reshape(128, -1))
    out = np.ascontiguousarray(np.concatenate(parts, axis=1), dtype=np.float32)
    assert out.shape == (128, WMIX_COLS)
    return out


def _const_tables(half):
    pos = (half * T + np.arange(T)).astype(np.float32)
    inv = (np.float32(500000.0) ** (-np.arange(0, 16, 2, dtype=np.float32) / np.float32(16.0))).astype(np.float32)
    ang = (pos[:, None] * inv[None, :]).astype(np.float32)
    c, s = np.cos(ang).astype(np.float32), np.sin(ang).astype(np.float32)
    cs = np.concatenate([c, c], axis=1).reshape(NT, 128, 16).transpose(1, 0, 2).reshape(128, 256)
    ss = np.concatenate([-s, s], axis=1).reshape(NT, 128, 16).transpose(1, 0, 2).reshape(128, 256)
    ik = np.arange(128)[:, None]
    iq = np.arange(128)[None, :]
    bandL = np.where(iq <= ik, 0.0, NEG).astype(np.float32)
    bandR = np.where(ik <= iq, 0.0, NEG).astype(np.float32)
    haloL = bandL if half == 1 else np.full((128, 128), NEG, np.float32)
    cb = np.concatenate([np.eye(128, dtype=np.float32), bandL, bandR, haloL, bandR], axis=1)
    invcnt = np.zeros((4, 16), np.float32)
    for gi, win in enumerate((2, 4, 8, 16)):
        j = np.arange(16)
        cnt = np.minimum(j + 1, win) if half == 0 else np.full(16, win)
        invcnt[gi] = 1.0 / cnt.astype(np.float32)
    return cs, ss, cb, invcnt


_NC_CACHE = {}


def kernel(x, ffn1_norm, ffn1_w1, ffn1_w3, ffn1_w2, mix_norm, w_in, w_branch_attn, w_branch_pool,
           pool_w, pool_scale, w_out, ffn2_norm, ffn2_w1, ffn2_w3, ffn2_w2, final_norm, _dbg=None):
    f = lambda a: np.asarray(a, dtype=np.float32)
    x = f(x)
    wf1 = _ffn_layout(f(ffn1_w1)[0], f(ffn1_w3)[0], f(ffn1_w2)[0])
    wf2 = _ffn_layout(f(ffn2_w1)[0], f(ffn2_w3)[0], f(ffn2_w2)[0])
    wmix = _mix_layout(f(w_in)[0], f(w_branch_attn)[0], f(w_branch_pool)[0], f(pool_w)[0], f(w_out)[0])
    gains = np.concatenate([f(ffn1_norm)[0], f(mix_norm)[0], f(ffn2_norm)[0], f(final_norm)])
    pscale = f(pool_scale)[0].reshape(4, 128).T
    in_maps = []
    for core in range(8):
        b, half = core // 2, core % 2
        cs, ss, cb, invcnt = _const_tables(half)
        cf = np.zeros((128, CF_COLS), np.float32)
        cf[:, CF_GAIN:CF_GAIN + 4096] = gains[None, :]
        cf[:, CF_CS:CF_CS + 256] = cs
        cf[:, CF_SS:CF_SS + 256] = ss
        cf[:, CF_SM:CF_SM + 4] = pscale
        cf[:, CF_SM + 4] = float(half)
        cf[:, CF_SM + 8:CF_SM + 72] = invcnt.reshape(1, 64)
        m = {"x": np.ascontiguousarray(x[b, half * T:(half + 1) * T, :]),
             "wmix": wmix, "cf": cf, "cb": np.ascontiguousarray(cb)}
        if not (_dbg and _dbg.startswith("x")):
            m["wf1"] = wf1
        if _dbg is None:
            m["wf2"] = wf2
        in_maps.append(m)
    key = _dbg
    if key not in _NC_CACHE:
        _NC_CACHE[key] = _build(_dbg)
    nc = _NC_CACHE[key]
    res = run_bass_kernel_spmd(nc, in_maps, core_ids=list(range(8)))
    out = np.empty((4, 4096, D), np.float32)
    for core in range(8):
        b, half = core // 2, core % 2
        out[b, half * T:(half + 1) * T, :] = np.asarray(res.results[core]["out"], dtype=np.float32)
    return out
```
